# Optimizing a Trainium2 kernel written in Bass

```python
import jax, jax.numpy as jnp
from jax import lax
import numpy as np

D_MODEL = 1024
BATCH = 8
SEQ = 2048
DEPTH = 4
DEC_BATCH = 128
DEC_SEQ = 8
PAST_LEN = 16384
PAGE_SIZE = 128

N_MIXERS = 2
N_A = (DEPTH + 1) // 2
N_B = DEPTH // 2
CONV_A_WIDTH = 31
CONV_B_WIDTH = 3
D_FF = -(-(8 * D_MODEL) // (3 * 256)) * 256
RMS_EPS = 1e-6
LN_EPS = 1e-5

kernel_name = 'hybrid_conformer_shortconv_decoder_step'


def rmsnorm(x, g):
    xf = x.astype(jnp.float32)
    y = xf * lax.rsqrt(jnp.mean(xf * xf, axis=-1, keepdims=True) + RMS_EPS)
    return (y * g.astype(jnp.float32)).astype(x.dtype)


def layernorm(x, g, b):
    xf = x.astype(jnp.float32)
    mu = jnp.mean(xf, axis=-1, keepdims=True)
    xc = xf - mu
    var = jnp.mean(xc * xc, axis=-1, keepdims=True)
    y = xc * lax.rsqrt(var + LN_EPS) * g.astype(jnp.float32) + b.astype(jnp.float32)
    return y.astype(x.dtype)


def causal_dwconv(h_ext, w):
    c = h_ext.shape[-1]
    return lax.conv_general_dilated(
        h_ext, w[:, None, :].astype(h_ext.dtype), window_strides=(1,), padding='VALID',
        dimension_numbers=('NWC', 'WIO', 'NWC'), feature_group_count=c)


def mixer_a(h, buf, w_in, b_in, w_dw, b_dw, ln_g, ln_b, w_out, b_out):
    u = h @ w_in + b_in
    a, g = jnp.split(u, 2, axis=-1)
    v = a * jax.nn.sigmoid(g)
    v_ext = jnp.concatenate([buf.astype(v.dtype), v], axis=1)
    c = causal_dwconv(v_ext, w_dw) + b_dw
    c = layernorm(c, ln_g, ln_b)
    y = jax.nn.silu(c) @ w_out + b_out
    return y, v_ext[:, -(CONV_A_WIDTH - 1):]


def mixer_b(h, buf, w_in, w_conv, w_out):
    u = h @ w_in
    bg, cg, v = jnp.split(u, 3, axis=-1)
    z = cg * v
    z_ext = jnp.concatenate([buf.astype(z.dtype), z], axis=1)
    c = causal_dwconv(z_ext, w_conv)
    y = (bg * c) @ w_out
    return y, z_ext[:, -(CONV_B_WIDTH - 1):]


def swiglu(h, w_gate_up, w_down):
    gate, up = jnp.split(h @ w_gate_up, 2, axis=-1)
    return (jax.nn.silu(gate) * up) @ w_down


def trunk(x, bufs_a, bufs_b, g_mix_pre, g_mix_post, g_ffn_pre, g_ffn_post,
          a_w_in, a_b_in, a_w_dw, a_b_dw, a_ln_g, a_ln_b, a_w_out, a_b_out,
          b_w_in, b_w_conv, b_w_out, w_gate_up, w_down):
    new_a, new_b = [], []
    for i in range(DEPTH):
        h = rmsnorm(x, g_mix_pre[i])
        j = i // N_MIXERS
        if i % N_MIXERS == 0:
            m, nb = mixer_a(h, bufs_a[j], a_w_in[j], a_b_in[j], a_w_dw[j], a_b_dw[j],
                            a_ln_g[j], a_ln_b[j], a_w_out[j], a_b_out[j])
            new_a.append(nb)
        else:
            m, nb = mixer_b(h, bufs_b[j], b_w_in[j], b_w_conv[j], b_w_out[j])
            new_b.append(nb)
        x = x + rmsnorm(m, g_mix_post[i])
        f = swiglu(rmsnorm(x, g_ffn_pre[i]), w_gate_up[i], w_down[i])
        x = x + rmsnorm(f, g_ffn_post[i])
    return x, jnp.stack(new_a), jnp.stack(new_b)


def setup_inputs(seed: int = 0) -> dict:
    key = jax.random.key(seed)
    ks = jax.random.split(key, 24)
    D, F = D_MODEL, D_FF
    n = lambda k, shape, s: jax.random.normal(k, shape, jnp.float32) * s
    return {
        'x_prompt': n(ks[0], (BATCH, SEQ, D), 1.0),
        'x_sample': n(ks[1], (DEC_BATCH, DEC_SEQ, D), 1.0),
        'state_conv_a': n(ks[2], (N_A, DEC_BATCH, CONV_A_WIDTH - 1, D), 0.5),
        'state_conv_b': n(ks[3], (N_B, DEC_BATCH, CONV_B_WIDTH - 1, D), 0.5),
        'g_mix_pre': 1.0 + n(ks[4], (DEPTH, D), 0.02),
        'g_mix_post': 1.0 + n(ks[5], (DEPTH, D), 0.02),
        'g_ffn_pre': 1.0 + n(ks[6], (DEPTH, D), 0.02),
        'g_ffn_post': 1.0 + n(ks[7], (DEPTH, D), 0.02),
        'a_w_in': n(ks[8], (N_A, D, 2 * D), D ** -0.5),
        'a_b_in': n(ks[9], (N_A, 2 * D), 0.02),
        'a_w_dw': n(ks[10], (N_A, CONV_A_WIDTH, D), CONV_A_WIDTH ** -0.5),
        'a_b_dw': n(ks[11], (N_A, D), 0.02),
        'a_ln_g': 1.0 + n(ks[12], (N_A, D), 0.02),
        'a_ln_b': n(ks[13], (N_A, D), 0.02),
        'a_w_out': n(ks[14], (N_A, D, D), D ** -0.5),
        'a_b_out': n(ks[15], (N_A, D), 0.02),
        'b_w_in': n(ks[16], (N_B, D, 3 * D), D ** -0.5),
        'b_w_conv': n(ks[17], (N_B, CONV_B_WIDTH, D), CONV_B_WIDTH ** -0.5),
        'b_w_out': n(ks[18], (N_B, D, D), D ** -0.5),
        'w_gate_up': n(ks[19], (DEPTH, D, 2 * F), D ** -0.5),
        'w_down': n(ks[20], (DEPTH, F, D), F ** -0.5),
    }


def reference(x_prompt, x_sample, state_conv_a, state_conv_b,
              g_mix_pre, g_mix_post, g_ffn_pre, g_ffn_post,
              a_w_in, a_b_in, a_w_dw, a_b_dw, a_ln_g, a_ln_b, a_w_out, a_b_out,
              b_w_in, b_w_conv, b_w_out, w_gate_up, w_down):
    weights = (g_mix_pre, g_mix_post, g_ffn_pre, g_ffn_post,
               a_w_in, a_b_in, a_w_dw, a_b_dw, a_ln_g, a_ln_b, a_w_out, a_b_out,
               b_w_in, b_w_conv, b_w_out, w_gate_up, w_down)
    zeros_a = jnp.zeros((N_A, x_prompt.shape[0], CONV_A_WIDTH - 1, D_MODEL), x_prompt.dtype)
    zeros_b = jnp.zeros((N_B, x_prompt.shape[0], CONV_B_WIDTH - 1, D_MODEL), x_prompt.dtype)
    y_prompt, new_conv_a_prompt, new_conv_b_prompt = trunk(x_prompt, zeros_a, zeros_b, *weights)
    y_sample, new_conv_a_sample, new_conv_b_sample = trunk(x_sample, state_conv_a, state_conv_b, *weights)
    return (y_prompt, y_sample, new_conv_a_prompt, new_conv_b_prompt, new_conv_a_sample, new_conv_b_sample)
```

```python
import numpy as np
import concourse.bass as bass
import concourse.mybir as mybir
from concourse.bass_utils import run_bass_kernel_spmd

F32 = mybir.dt.float32
BF16 = mybir.dt.bfloat16
AF = mybir.ActivationFunctionType
ALU = mybir.AluOpType

D = 1024
DC = 8
FF = 2816
FC = 22
NCORE = 8
TG = 1088
NG = 2
TILES = [(0, 512), (512, 512), (1024, 64)]
KA, KB = 31, 3
RMS_EPS = 1e-6
LN_EPS = 1e-5
NSLOT = 6
SLOT_ELEMS = 3072

VEC_SPEC = [
    ("g_mix_pre", (4, 8)), ("g_mix_post", (4, 8)), ("g_ffn_pre", (4, 8)), ("g_ffn_post", (4, 8)),
    ("a_b_in", (2, 16)), ("a_w_dw", (2, 8, 31)), ("a_b_dw", (2, 8)), ("a_ln_g", (2, 8)),
    ("a_ln_b", (2, 8)), ("a_b_out", (2, 8)), ("b_w_conv", (2, 8, 3)),
]
VEC_OFF = {}
_o = 0
for _n, _s in VEC_SPEC:
    VEC_OFF[_n] = (_o, _s)
    _o += int(np.prod(_s))
NV = _o


class Sched:
    ENG = ("pe", "act", "dve", "pool", "sp")

    def __init__(self):
        self.streams = {e: [] for e in self.ENG}
        self.count = {}
        self.seen = {e: {} for e in self.ENG}
        self.last_w = {}
        self.readers = {}

    def _need(self, eng, tok, waits):
        sem, val, _ = tok
        if self.seen[eng].get(sem, 0) >= val:
            return
        if waits.get(sem, 0) < val:
            waits[sem] = val

    def _resolve(self, eng, reads, writes):
        waits = {}
        for k in reads:
            w = self.last_w.get(k)
            if w is not None:
                self._need(eng, w, waits)
        for k in writes:
            w = self.last_w.get(k)
            if w is not None and w[2] != eng:
                self._need(eng, w, waits)
            for r in self.readers.get(k, {}).values():
                if r[2] != eng:
                    self._need(eng, r, waits)
        for sem, val in waits.items():
            self.streams[eng].append(("wait", sem, val))
            self.seen[eng][sem] = val

    def _commit(self, tok, reads, writes):
        for k in reads:
            self.readers.setdefault(k, {})[tok[0]] = tok
        for k in writes:
            self.last_w[k] = tok
            self.readers[k] = {}

    def group(self, eng, fns, reads=(), writes=()):
        self._resolve(eng, reads, writes)
        self.count[eng] = self.count.get(eng, 0) + 1
        tok = (eng, self.count[eng], eng)
        for f in fns[:-1]:
            self.streams[eng].append(("op", f, None, 0))
        self.streams[eng].append(("op", fns[-1], eng, 1))
        self._commit(tok, reads, writes)
        return tok

    def op(self, eng, fn, reads=(), writes=()):
        return self.group(eng, [fn], reads, writes)

    def dma(self, q, sem, fn, reads=(), writes=()):
        self._resolve(q, reads, writes)
        prev = self.count.get(sem, 0)
        if prev and self.seen[q].get(sem, 0) < prev:
            self.streams[q].append(("wait", sem, prev))
            self.seen[q][sem] = prev
        self.count[sem] = prev + 16
        tok = (sem, prev + 16, None)
        self.streams[q].append(("op", fn, sem, 16))
        self._commit(tok, reads, writes)
        return tok

    def wait_tok(self, eng, tok):
        if self.seen[eng].get(tok[0], 0) < tok[1]:
            self.streams[eng].append(("wait", tok[0], tok[1]))
            self.seen[eng][tok[0]] = tok[1]


def build_program(n_layers=4, n_groups=NG):
    nc = bass.Bass("TRN2", target_bir_lowering=False)

    def din(name, shape):
        return nc.dram_tensor(name, list(shape), F32, kind="ExternalInput").ap()

    def dout(name, shape):
        return nc.dram_tensor(name, list(shape), F32, kind="ExternalOutput").ap()

    xin = din("xin", (NG, 128, DC, TG))
    sa = din("sa", (2, NG, 128, DC, 240))
    sb = din("sb", (2, NG, 128, DC, 16))
    vecs_d = din("vecs", (128, NV))
    ident_d = din("ident", (128, 128))
    wA_in = din("wA_in", (2, 8, 128, 2048))
    wA_out = din("wA_out", (2, 4, 128, 2048))
    wB_in = din("wB_in", (2, 8, 128, 3072))
    wB_out = din("wB_out", (2, 4, 128, 2048))
    wgu = din("wgu", (4, FC, 128, 2048))
    wdn = din("wdn", (4, 8, 128, FF))

    yT = dout("yT", (NG, 128, DC, TG))
    nca_p = dout("nca_p", (2, 128, DC, 30))
    ncb_p = dout("ncb_p", (2, 128, DC, 2))
    nca_s = dout("nca_s", (2, NG, 128, DC, 240))
    ncb_s = dout("ncb_s", (2, NG, 128, DC, 16))

    S = Sched()
    from contextlib import ExitStack
    es = ExitStack()

    def sb_t(name, shape, dt):
        return es.enter_context(nc.sbuf_tensor(name, list(shape), dt))

    with es:
        xT = sb_t("xT", (128, DC, TG), F32)
        hT = sb_t("hT", (128, DC, TG), BF16)
        cm = sb_t("cm", (128, DC, TG), F32)
        hid = sb_t("hid", (128, FC, TG), BF16)
        wsl = [sb_t(f"wsl{i}", (128, SLOT_ELEMS), BF16) for i in range(NSLOT)]
        sqb = [sb_t(f"sqb{i}", (128, DC, 512), BF16) for i in range(2)]
        sqb.append(sb_t("sqb2", (128, DC, 64), BF16))
        fsc = [sb_t(f"fsc{i}", (128, 512), F32)[:] for i in range(7)]
        fsc.append(fsc[4][:, 256:512])
        vecs = sb_t("vecs_sb", (128, NV), F32)
        ident_f = sb_t("ident_f", (128, 128), F32)
        ident_b = sb_t("ident_b", (128, 128), BF16)
        ones_b = sb_t("ones_b", (128, 128), BF16)
        eps_rms = sb_t("eps_rms", (128, 1), F32)
        eps_ln = sb_t("eps_ln", (128, 1), F32)
        dummy = sb_t("act_dummy", (128, 1), F32)
        vtail = sb_t("vtail", (128, DC, 30), F32)
        vnew_s = sb_t("vnew_s", (128, DC, 64), F32)
        ztail = sb_t("ztail", (128, DC, 2), F32)
        znew_s = sb_t("znew_s", (128, DC, 16), F32)
        histA = sb_t("histA", (128, 2, DC, 30), BF16)
        histB = sb_t("histB", (128, 2, DC, 2), BF16)
        ps = es.enter_context(nc.psum_tensor("ps", [128, 8, 512], F32))

        hid_flat = hid[:].rearrange("p a b -> p (a b)")
        VP_W = 1024 + 30
        VS_W = 240 + 64
        o0 = 0
        vT_p = hid_flat[:, o0:o0 + DC * VP_W].rearrange("p (c n) -> p c n", n=VP_W)
        o0 += DC * VP_W
        vT_s = hid_flat[:, o0:o0 + DC * VS_W].rearrange("p (c n) -> p c n", n=VS_W)
        o0 += DC * VS_W
        diag = []
        for i in range(3):
            diag.append(hid_flat[:, o0:o0 + KA * 128].rearrange("p (k n) -> p k n", n=128))
            o0 += KA * 128
        assert o0 <= FC * TG

        sem_names = list(Sched.ENG) + [f"w{i}" for i in range(NSLOT)] + [f"io{i}" for i in range(8)]
        sems = {n: es.enter_context(nc.semaphore(n)) for n in sem_names}

        def vec(name, *idx):
            off, shp = VEC_OFF[name]
            strides = [int(np.prod(shp[i + 1:])) for i in range(len(shp))]
            o = off + sum(i * s for i, s in zip(idx, strides))
            n = strides[len(idx) - 1] if len(idx) else int(np.prod(shp))
            if len(idx) == len(shp):
                n = 1
            return vecs[:, o:o + n]

        st = {"bank": 0, "piece": 0, "io": 0, "fs": 0, "sq": 0, "dg": 0}

        def next_bank():
            b = st["bank"]
            st["bank"] = (b + 1) % 8
            return b

        def next_io():
            i = st["io"]
            st["io"] = (i + 1) % 8
            return f"io{i}"

        def next_fs(lo=0, hi=2):
            k = ("fsrot", lo, hi)
            i = st.get(k, lo)
            st[k] = lo + ((i - lo + 1) % (hi - lo))
            return i

        def next_sq():
            i = st["sq"]
            st["sq"] = (i + 1) % 2
            return i

        out_toks = []

        def load_piece(dram_ap, n):
            s = st["piece"] % NSLOT
            st["piece"] += 1
            dst = wsl[s][:, 0:n]
            S.dma("pool", f"w{s}", lambda e, d=dst, a=dram_ap: e.dma_start(out=d, in_=a),
                  reads=(), writes=(("w", s),))
            return s

        def mm_group(psap, pairs, reads, writes):
            n = len(pairs)
            fns = []
            for i, (l, r) in enumerate(pairs):
                fns.append(lambda e, l=l, r=r, i=i: e.matmul(psap, l, r, start=(i == 0), stop=(i == n - 1)))
            return S.group("pe", fns, reads, writes)

        S.dma("sp", next_io(), lambda e: e.dma_start(out=vecs[:], in_=vecs_d), writes=(("vec",),))
        S.dma("sp", next_io(), lambda e: e.dma_start(out=ident_f[:], in_=ident_d), writes=(("identf",),))
        S.op("dve", lambda e: e.tensor_copy(out=ident_b[:], in_=ident_f[:]), reads=(("identf",),), writes=(("ident",),))
        S.op("dve", lambda e: e.memset(ones_b[:], 1.0), writes=(("ones",),))
        S.op("dve", lambda e: e.memset(eps_rms[:], float(D * RMS_EPS)), writes=(("eps",),))
        S.op("dve", lambda e: e.memset(eps_ln[:], float(LN_EPS)), writes=(("eps",),))
        S.op("dve", lambda e: e.tensor_scalar(out=vecs[:, 0:128], in0=vecs[:, 0:128], scalar1=float(np.sqrt(D)),
                                              scalar2=None, op0=ALU.mult),
             reads=(("vec",),), writes=(("vec",),))

        NT = len(TILES)
        def rms_stats(src_key_reads, sq_i, W):
            b = next_bank()
            pairs = [(ones_b[:], sqb[sq_i][:, c, 0:W]) for c in range(DC)]
            mm_group(ps[:, b, 0:W], pairs, reads=(("sq", sq_i), ("ones",)), writes=(("ps", b),))
            return b

        def stat_sqrt(src_sq_fn, src_reads, t, fi, eng="act", defer=False):
            c0, W = TILES[t]
            qi = next_sq()
            S.op(eng, lambda e: src_sq_fn(e, sqb[qi][:, :, 0:W]), reads=src_reads, writes=(("sq", qi),))
            b = rms_stats(None, qi, W)

            def fin():
                S.op("act", lambda e: e.activation(out=fsc[fi][:, 0:W], in_=ps[:, b, 0:W], func=AF.Sqrt, bias=eps_rms[:, 0:1]),
                     reads=(("ps", b), ("eps",)), writes=(("f", fi),))
            if defer:
                return fin
            fin()
            return None

        def recip(t, fi):
            c0, W = TILES[t]
            S.op("dve", lambda e: e.reciprocal(out=fsc[fi][:, 0:W], in_=fsc[fi][:, 0:W]),
                 reads=(("f", fi),), writes=(("f", fi),))

        def xk(t):
            return tuple(("x", t, c) for c in range(DC))

        def stage_x_stats(t, fi, defer=False):
            c0, W = TILES[t]
            return stat_sqrt(lambda e, o: e.activation(out=o, in_=xT[:, :, c0:c0 + W], func=AF.Square), xk(t), t, fi,
                             defer=defer)

        def stage_cm_stats(t, fi):
            c0, W = TILES[t]
            cmkeys = tuple(("cm", d, t) for d in range(DC))
            if t == 1:
                stat_sqrt(lambda e, o: e.tensor_tensor(out=o, in0=cm[:, :, c0:c0 + W], in1=cm[:, :, c0:c0 + W], op=ALU.mult),
                          cmkeys, t, fi, eng="dve")
            else:
                stat_sqrt(lambda e, o: e.activation(out=o, in_=cm[:, :, c0:c0 + W], func=AF.Square), cmkeys, t, fi)

        def stage_h(gname, l, t):
            c0, W = TILES[t]
            for c in range(DC):
                S.op("act", lambda e, c=c: e.activation(out=hT[:, c, c0:c0 + W], in_=xT[:, c, c0:c0 + W],
                                                        func=AF.Copy, scale=vec(gname, l, c)),
                     reads=(("x", t, c), ("vec",)), writes=(("h", t),))

        def stage_resid(gname, l, t, fi, pre_g=None, l2=None):
            c0, W = TILES[t]

            def stt(c):
                S.op("dve", lambda e: e.scalar_tensor_tensor(
                    out=cm[:, c, c0:c0 + W], in0=cm[:, c, c0:c0 + W], scalar=vec(gname, l, c),
                    in1=fsc[fi][:, 0:W], op0=ALU.mult, op1=ALU.mult),
                    reads=(("cm", c, t), ("f", fi), ("vec",)), writes=(("cm", c, t),))

            def add_h(c):
                S.op("dve", lambda e: e.tensor_tensor(out=xT[:, c, c0:c0 + W], in0=xT[:, c, c0:c0 + W],
                                                      in1=cm[:, c, c0:c0 + W], op=ALU.add),
                     reads=(("cm", c, t), ("x", t, c)), writes=(("x", t, c),))
                if pre_g is not None:
                    S.op("act", lambda e: e.activation(out=hT[:, c, c0:c0 + W], in_=xT[:, c, c0:c0 + W],
                                                       func=AF.Copy, scale=vec(pre_g, l2, c)),
                         reads=(("x", t, c), ("vec",)), writes=(("h", t),))

            for c in range(DC):
                stt(c)
                if c >= 1:
                    add_h(c - 1)
            add_h(DC - 1)

        R1 = [2, 3, 4]
        R2 = [5, 6, 7]

        def rb(t):
            c0, W = TILES[t]
            return fsc[R2[t]][:, 0:W]

        pend_x = {}

        def x_square(t):
            c0, W = TILES[t]
            S.op("act", lambda e: e.activation(out=sqb[t][:, :, 0:W], in_=xT[:, :, c0:c0 + W], func=AF.Square),
                 reads=xk(t), writes=(("sq", t),))
            pend_x[t] = True

        def need_r(t):
            if pend_x.get(t):
                pend_x[t] = False
                c0, W = TILES[t]
                b = rms_stats(None, t, W)
                fi = R2[t]
                S.op("act", lambda e: e.activation(out=fsc[fi][:, 0:W], in_=ps[:, b, 0:W], func=AF.Sqrt, bias=eps_rms[:, 0:1]),
                     reads=(("ps", b), ("eps",)), writes=(("f", fi),))
                recip(t, fi)

        def rkey(t):
            return ("f", R2[t])

        def first_prenorm(gname, l):
            for t in range(NT):
                stage_h(gname, l, t)
                x_square(t)

        def boundary(post_g, l, pre_g, l2, g):
            for t in range(NT):
                stage_cm_stats(t, R1[t])
                if t == 0:
                    S.op("act", lambda e: e.activation(out=dummy[:, 0:1], in_=eps_ln[:, 0:1], func=AF.Sqrt),
                         reads=(("eps",),), writes=(("dummy",),))
            for t in range(NT):
                recip(t, R1[t])
                stage_resid(post_g, l, t, R1[t], pre_g, l2)
                if pre_g is not None:
                    x_square(t)
                else:
                    c0, W = TILES[t]
                    for c in range(DC):
                        out_toks.append(S.dma("sp", next_io(), lambda e, c=c, c0=c0, W=W: e.dma_start(
                            out=yT[g][:, c, c0:c0 + W], in_=xT[:, c, c0:c0 + W]), reads=(("x", t, c),)))

        def build_diag(wname, l, j, K):
            di = st["dg"]
            st["dg"] = 1 - di
            dst = diag[di][:, 0:K, :]
            w_ap = vec(wname, l, j)
            in0 = ident_b[:].unsqueeze(1).broadcast_to([128, K, 128])
            in1 = w_ap.unsqueeze(2).broadcast_to([128, K, 128])
            S.op("pool", lambda e: e.tensor_tensor(out=dst, in0=in0, in1=in1, op=ALU.mult),
                 reads=(("ident",), ("vec",)), writes=(("diag", di), ("hidalias",)))
            return di

        def conv_rhs(j, t, k, hist):
            c0, W = TILES[t]
            if t < 2:
                return vT_p[:, j, c0 + k:c0 + k + W]
            return vT_s[:, j, 8 * k:8 * k + 64]

        def conv_reads(j, t):
            if t == 0:
                return (("v", j, "hist"), ("v", j, 0))
            if t == 1:
                return (("v", j, 0), ("v", j, 1))
            return (("v", j, "shist"), ("v", j, 2))

        def vdst(j, t, hist):
            c0, W = TILES[t]
            if t < 2:
                return vT_p[:, j, hist + c0:hist + c0 + W]
            return vT_s[:, j, 8 * hist:8 * hist + 64]

        def init_hist(kind, lm, g, hist, hbuf, s_in):
            keys = tuple(("v", j, "hist") for j in range(DC))
            if g == 0:
                S.op("dve", lambda e: e.memset(vT_p[:, :, 0:hist], 0.0), writes=keys + (("hidalias",),))
            else:
                S.op("dve", lambda e: e.tensor_copy(out=vT_p[:, :, 0:hist], in_=hbuf[:, lm, :, :]),
                     reads=(("hist", kind, lm),), writes=keys + (("hidalias",),))

        def init_hist_pool(lm, g, hist, s_in):
            skeys = tuple(("v", j, "shist") for j in range(DC))
            S.dma("pool", next_io(), lambda e: e.dma_start(out=vT_s[:, :, 0:8 * hist], in_=s_in[lm, g]),
                  writes=skeys + (("hidalias",),))

        BS = 3

        def blocks(n, bs=BS):
            return [list(range(b0, min(b0 + bs, n))) for b0 in range(0, n, bs)]

        def hmm(psb, W, w3, lo, hi, c0, s, t):
            mm_group(ps[:, psb, 0:W], [(w3[:, kc, lo:hi], hT[:, kc, c0:c0 + W]) for kc in range(DC)],
                     reads=(("w", s), ("h", t)), writes=(("ps", psb),))

        def ffn(l, g):
            for blk in blocks(FC):
                slots = {j: load_piece(wgu[l, j], 2048) for j in blk}
                for t, (c0, W) in enumerate(TILES):
                    for j in blk:
                        s = slots[j]
                        w3 = wsl[s][:, 0:2048].rearrange("p (k n) -> p k n", n=256)
                        bg = next_bank()
                        hmm(bg, W, w3, 0, 128, c0, s, t)
                        bu = next_bank()
                        hmm(bu, W, w3, 128, 256, c0, s, t)
                        need_r(t)
                        f0 = next_fs(0, 4)
                        f1 = next_fs(0, 4)
                        S.op("dve", lambda e, bg=bg, f0=f0, W=W, t=t: e.tensor_tensor(
                            out=fsc[f0][:, 0:W], in0=ps[:, bg, 0:W], in1=rb(t), op=ALU.mult),
                            reads=(("ps", bg), rkey(t)), writes=(("f", f0),))
                        S.op("dve", lambda e, bu=bu, f1=f1, W=W, t=t: e.tensor_tensor(
                            out=fsc[f1][:, 0:W], in0=ps[:, bu, 0:W], in1=rb(t), op=ALU.mult),
                            reads=(("ps", bu), rkey(t)), writes=(("f", f1),))
                        S.op("act", lambda e, f0=f0, W=W: e.activation(out=fsc[f0][:, 0:W], in_=fsc[f0][:, 0:W], func=AF.Silu),
                             reads=(("f", f0),), writes=(("f", f0),))
                        S.op("dve", lambda e, f0=f0, f1=f1, W=W, j=j, c0=c0: e.tensor_tensor(
                            out=hid[:, j, c0:c0 + W], in0=fsc[f1][:, 0:W], in1=fsc[f0][:, 0:W], op=ALU.mult),
                            reads=(("f", f0), ("f", f1)), writes=(("hid", j, t),))
            for blk in blocks(DC):
                slots = {d: load_piece(wdn[l, d], FF) for d in blk}
                for t, (c0, W) in enumerate(TILES):
                    for d in blk:
                        s = slots[d]
                        w3 = wsl[s][:, 0:FF].rearrange("p (k n) -> p k n", n=128)
                        b = next_bank()
                        mm_group(ps[:, b, 0:W], [(w3[:, kc, :], hid[:, kc, c0:c0 + W]) for kc in range(FC)],
                                 reads=(("w", s), ("hidalias",)) + tuple(("hid", kc, t) for kc in range(FC)),
                                 writes=(("ps", b),))
                        S.op("act", lambda e, b=b, d=d, c0=c0, W=W: e.activation(out=cm[:, d, c0:c0 + W], in_=ps[:, b, 0:W], func=AF.Copy),
                             reads=(("ps", b),), writes=(("cm", d, t),))

        def w_out_phase(wdram, lm, bias_name):
            blk = list(range(4))
            slots = {p: load_piece(wdram[lm, p], 2048) for p in blk}
            for t, (c0, W) in enumerate(TILES):
                for p in blk:
                    s = slots[p]
                    w3 = wsl[s][:, 0:2048].rearrange("p (k n) -> p k n", n=256)
                    for dd in range(2):
                        d = 2 * p + dd
                        b = next_bank()
                        hmm(b, W, w3, dd * 128, (dd + 1) * 128, c0, s, t)
                        if bias_name is not None:
                            S.op("act", lambda e, b=b, d=d, c0=c0, W=W: e.activation(
                                out=cm[:, d, c0:c0 + W], in_=ps[:, b, 0:W], func=AF.Identity, bias=vec(bias_name, lm, d)),
                                reads=(("ps", b), ("vec",)), writes=(("cm", d, t),))
                        else:
                            S.op("act", lambda e, b=b, d=d, c0=c0, W=W: e.activation(out=cm[:, d, c0:c0 + W], in_=ps[:, b, 0:W], func=AF.Copy),
                                 reads=(("ps", b),), writes=(("cm", d, t),))

        def layer_a(l, g):
            la = l // 2
            init_hist("a", la, g, 30, histA, sa)

            def build_a(j, di):
                dst = diag[di][:, 0:KA, :]
                in0 = ident_b[:].unsqueeze(1).broadcast_to([128, KA, 128])
                in1 = vec("a_w_dw", la, j).unsqueeze(2).broadcast_to([128, KA, 128])
                S.op("dve", lambda e: e.tensor_tensor(out=dst, in0=in0, in1=in1, op=ALU.mult),
                     reads=(("ident",), ("vec",)), writes=(("diag", di), ("hidalias",)))

            def conv_block(blk):
                for t, (c0, W) in enumerate(TILES):
                    for j in blk:
                        di = j % 3
                        b = next_bank()
                        mm_group(ps[:, b, 0:W], [(diag[di][:, k, :], conv_rhs(j, t, k, 30)) for k in range(KA)],
                                 reads=(("diag", di),) + conv_reads(j, t), writes=(("ps", b),))
                        S.op("act", lambda e, b=b, j=j, c0=c0, W=W: e.activation(
                            out=cm[:, j, c0:c0 + W], in_=ps[:, b, 0:W], func=AF.Identity, bias=vec("a_b_dw", la, j)),
                            reads=(("ps", b), ("vec",)), writes=(("cm", j, t),))

            def win_block(blk):
                slots = {j: load_piece(wA_in[la, j], 2048) for j in blk}
                for t, (c0, W) in enumerate(TILES):
                    for j in blk:
                        s = slots[j]
                        w3 = wsl[s][:, 0:2048].rearrange("p (k n) -> p k n", n=256)
                        ba = next_bank()
                        hmm(ba, W, w3, 0, 128, c0, s, t)
                        bgk = next_bank()
                        hmm(bgk, W, w3, 128, 256, c0, s, t)
                        need_r(t)
                        fi = next_fs(0, 4)
                        f1 = next_fs(0, 4)
                        S.op("dve", lambda e, bgk=bgk, fi=fi, W=W, t=t: e.tensor_tensor(
                            out=fsc[fi][:, 0:W], in0=ps[:, bgk, 0:W], in1=rb(t), op=ALU.mult),
                            reads=(("ps", bgk), rkey(t)), writes=(("f", fi),))
                        S.op("dve", lambda e, ba=ba, f1=f1, W=W, t=t: e.tensor_tensor(
                            out=fsc[f1][:, 0:W], in0=ps[:, ba, 0:W], in1=rb(t), op=ALU.mult),
                            reads=(("ps", ba), rkey(t)), writes=(("f", f1),))
                        S.op("act", lambda e, fi=fi, W=W, j=j: e.activation(
                            out=fsc[fi][:, 0:W], in_=fsc[fi][:, 0:W], func=AF.Sigmoid, bias=vec("a_b_in", la, 8 + j)),
                            reads=(("f", fi), ("vec",)), writes=(("f", fi),))
                        fns = [lambda e, f1=f1, fi=fi, W=W, j=j, t=t: e.scalar_tensor_tensor(
                            out=vdst(j, t, 30), in0=fsc[f1][:, 0:W], scalar=vec("a_b_in", la, j),
                            in1=fsc[fi][:, 0:W], op0=ALU.add, op1=ALU.mult)]
                        wr = [("v", j, t)]
                        if t == 1 and g == 1:
                            fns.append(lambda e, f1=f1, fi=fi, j=j: e.scalar_tensor_tensor(
                                out=vtail[:, j, :], in0=fsc[f1][:, 482:512], scalar=vec("a_b_in", la, j),
                                in1=fsc[fi][:, 482:512], op0=ALU.add, op1=ALU.mult))
                            wr.append(("vtail", j))
                        if t == 2:
                            fns.append(lambda e, f1=f1, fi=fi, j=j: e.scalar_tensor_tensor(
                                out=vnew_s[:, j, :], in0=fsc[f1][:, 0:64], scalar=vec("a_b_in", la, j),
                                in1=fsc[fi][:, 0:64], op0=ALU.add, op1=ALU.mult))
                            wr.append(("vnew", j))
                        S.group("dve", fns, reads=(("f", f1), ("f", fi), ("vec",)), writes=tuple(wr))
                        if t == 1 and g == 0:
                            S.op("dve", lambda e, j=j: e.tensor_copy(out=histA[:, la, j, :], in_=vT_p[:, j, 1024:1054]),
                                 reads=(("v", j, 1),), writes=(("hist", "a", la),))

            blks = blocks(DC)
            for j in blks[0]:
                build_a(j, j % 3)
            win_block(blks[0])
            init_hist_pool(la, g, 30, sa)
            for bi in range(1, len(blks)):
                win_block(blks[bi])
                conv_block(blks[bi - 1])
                for j in blks[bi]:
                    build_a(j, j % 3)
            conv_block(blks[-1])
            if g == 1:
                out_toks.append(S.dma("sp", next_io(), lambda e: e.dma_start(out=nca_p[la], in_=vtail[:]),
                                      reads=tuple(("vtail", j) for j in range(DC))))
            out_toks.append(S.dma("sp", next_io(), lambda e: e.dma_start(out=nca_s[la, g][:, :, 176:240], in_=vnew_s[:]),
                                  reads=tuple(("vnew", j) for j in range(DC))))
            out_toks.append(S.dma("sp", next_io(), lambda e: e.dma_start(out=nca_s[la, g][:, :, 0:176],
                                                                         in_=sa[la, g][:, :, 64:240])))
            MU, RS = R1, R2
            lnb = {}
            for t, (c0, W) in enumerate(TILES):
                cmkeys = tuple(("cm", j, t) for j in range(DC))
                q1 = next_sq()
                S.op("dve", lambda e, q1=q1, c0=c0, W=W: e.tensor_copy(out=sqb[q1][:, :, 0:W], in_=cm[:, :, c0:c0 + W]),
                     reads=cmkeys, writes=(("sq", q1),))
                b1 = rms_stats(None, q1, W)
                q2 = next_sq()
                S.op("act", lambda e, q2=q2, c0=c0, W=W: e.activation(out=sqb[q2][:, :, 0:W], in_=cm[:, :, c0:c0 + W], func=AF.Square),
                     reads=cmkeys, writes=(("sq", q2),))
                b2 = rms_stats(None, q2, W)
                lnb[t] = (b1, b2)
                fmu, frs = MU[t], RS[t]
                S.op("dve", lambda e, b1=b1, W=W, fmu=fmu: e.tensor_scalar(out=fsc[fmu][:, 0:W], in0=ps[:, b1, 0:W], scalar1=1.0 / D,
                                                                          scalar2=None, op0=ALU.mult),
                     reads=(("ps", b1),), writes=(("f", fmu),))
                S.op("dve", lambda e, W=W, fmu=fmu, frs=frs: e.tensor_tensor(out=fsc[frs][:, 0:W], in0=fsc[fmu][:, 0:W], in1=fsc[fmu][:, 0:W], op=ALU.mult),
                     reads=(("f", fmu),), writes=(("f", frs),))
                S.op("dve", lambda e, b2=b2, W=W, frs=frs: e.scalar_tensor_tensor(out=fsc[frs][:, 0:W], in0=ps[:, b2, 0:W], scalar=1.0 / D,
                                                                                 in1=fsc[frs][:, 0:W], op0=ALU.mult, op1=ALU.subtract),
                     reads=(("ps", b2), ("f", frs)), writes=(("f", frs),))
                S.op("act", lambda e, W=W, frs=frs: e.activation(out=fsc[frs][:, 0:W], in_=fsc[frs][:, 0:W], func=AF.Sqrt, bias=eps_ln[:, 0:1]),
                     reads=(("f", frs), ("eps",)), writes=(("f", frs),))
            for t, (c0, W) in enumerate(TILES):
                cmkeys = tuple(("cm", j, t) for j in range(DC))
                fmu, frs = MU[t], RS[t]
                S.op("dve", lambda e, W=W, frs=frs: e.reciprocal(out=fsc[frs][:, 0:W], in_=fsc[frs][:, 0:W]),
                     reads=(("f", frs),), writes=(("f", frs),))
                def ln_sub(j, c0=c0, W=W, fmu=fmu, t=t):
                    S.op("dve", lambda e: e.tensor_tensor(out=cm[:, j, c0:c0 + W], in0=cm[:, j, c0:c0 + W],
                                                          in1=fsc[fmu][:, 0:W], op=ALU.subtract),
                         reads=(("cm", j, t), ("f", fmu)), writes=(("cm", j, t),))

                def ln_mul_silu(j, c0=c0, W=W, frs=frs, t=t):
                    S.op("dve", lambda e: e.tensor_tensor(out=cm[:, j, c0:c0 + W], in0=cm[:, j, c0:c0 + W],
                                                          in1=fsc[frs][:, 0:W], op=ALU.mult),
                         reads=(("cm", j, t), ("f", frs)), writes=(("cm", j, t),))
                    S.op("act", lambda e: e.activation(
                        out=hT[:, j, c0:c0 + W], in_=cm[:, j, c0:c0 + W], func=AF.Silu,
                        bias=vec("a_ln_b", la, j), scale=vec("a_ln_g", la, j)),
                        reads=(("cm", j, t), ("vec",)), writes=(("h", t),))

                for j in range(DC):
                    ln_sub(j)
                    if j >= 1:
                        ln_mul_silu(j - 1)
                ln_mul_silu(DC - 1)
            w_out_phase(wA_out, la, "a_b_out")

        def layer_b(l, g):
            lb = l // 2
            init_hist("b", lb, g, 2, histB, sb)
            dB = diag[0][:, 0:DC * KB, :]
            in0 = ident_b[:].unsqueeze(1).broadcast_to([128, DC * KB, 128])
            off, _ = VEC_OFF["b_w_conv"]
            wv = vecs[:, off + lb * DC * KB: off + (lb + 1) * DC * KB]
            in1 = wv.unsqueeze(2).broadcast_to([128, DC * KB, 128])
            r2done = {}
            for bi_, blk in enumerate(blocks(DC)):
                if bi_ == 1:
                    init_hist_pool(lb, g, 2, sb)
                    S.op("pool", lambda e: e.tensor_tensor(out=dB, in0=in0, in1=in1, op=ALU.mult),
                         reads=(("ident",), ("vec",)), writes=(("diag", 0), ("hidalias",)))
                slots = {j: load_piece(wB_in[lb, j], 3072) for j in blk}
                for t, (c0, W) in enumerate(TILES):
                    for j in blk:
                        s = slots[j]
                        w3 = wsl[s][:, 0:3072].rearrange("p (k n) -> p k n", n=384)
                        bc = next_bank()
                        hmm(bc, W, w3, 0, 128, c0, s, t)
                        bv = next_bank()
                        hmm(bv, W, w3, 128, 256, c0, s, t)
                        bb = next_bank()
                        hmm(bb, W, w3, 256, 384, c0, s, t)
                        need_r(t)
                        if not r2done.get(t):
                            r2done[t] = True
                            S.op("dve", lambda e, t=t, W=W: e.tensor_tensor(out=fsc[R1[t]][:, 0:W], in0=rb(t), in1=rb(t), op=ALU.mult),
                                 reads=(rkey(t),), writes=(("f", R1[t]),))
                        fi = next_fs(0, 2)
                        f1 = next_fs(0, 2)
                        S.op("act", lambda e, bc=bc, fi=fi, W=W: e.activation(out=fsc[fi][:, 0:W], in_=ps[:, bc, 0:W], func=AF.Copy),
                             reads=(("ps", bc),), writes=(("f", fi),))
                        S.op("dve", lambda e, bb=bb, j=j, c0=c0, W=W, t=t: e.tensor_tensor(
                            out=cm[:, j, c0:c0 + W], in0=ps[:, bb, 0:W], in1=rb(t), op=ALU.mult),
                            reads=(("ps", bb), rkey(t)), writes=(("cm", j, t),))
                        S.op("dve", lambda e, bv=bv, fi=fi, f1=f1, W=W: e.tensor_tensor(
                            out=fsc[f1][:, 0:W], in0=ps[:, bv, 0:W], in1=fsc[fi][:, 0:W], op=ALU.mult),
                            reads=(("ps", bv), ("f", fi)), writes=(("f", f1),))
                        r2 = fsc[R1[t]]
                        fns = [lambda e, f1=f1, W=W, j=j, t=t, r2=r2: e.tensor_tensor(
                            out=vdst(j, t, 2), in0=fsc[f1][:, 0:W], in1=r2[:, 0:W], op=ALU.mult)]
                        wr = [("v", j, t)]
                        if t == 1 and g == 1:
                            fns.append(lambda e, f1=f1, j=j, r2=r2: e.tensor_tensor(
                                out=ztail[:, j, :], in0=fsc[f1][:, 510:512], in1=r2[:, 510:512], op=ALU.mult))
                            wr.append(("ztail", j))
                        if t == 2:
                            fns.append(lambda e, f1=f1, j=j, r2=r2: e.tensor_tensor(
                                out=znew_s[:, j, :], in0=fsc[f1][:, 48:64], in1=r2[:, 48:64], op=ALU.mult))
                            wr.append(("znew", j))
                        S.group("dve", fns, reads=(("f", f1), ("f", R1[t])), writes=tuple(wr))
                        if t == 1 and g == 0:
                            S.op("dve", lambda e, j=j: e.tensor_copy(out=histB[:, lb, j, :], in_=vT_p[:, j, 1024:1026]),
                                 reads=(("v", j, 1),), writes=(("hist", "b", lb),))
            if g == 1:
                out_toks.append(S.dma("sp", next_io(), lambda e: e.dma_start(out=ncb_p[lb], in_=ztail[:]),
                                      reads=tuple(("ztail", j) for j in range(DC))))
            out_toks.append(S.dma("sp", next_io(), lambda e: e.dma_start(out=ncb_s[lb, g], in_=znew_s[:]),
                                  reads=tuple(("znew", j) for j in range(DC))))
            for t, (c0, W) in enumerate(TILES):
                for j in range(DC):
                    b = next_bank()
                    mm_group(ps[:, b, 0:W], [(dB[:, j * KB + k, :], conv_rhs(j, t, k, 2)) for k in range(KB)],
                             reads=(("diag", 0),) + conv_reads(j, t), writes=(("ps", b),))
                    S.op("dve", lambda e, b=b, j=j, c0=c0, W=W: e.tensor_tensor(
                        out=hT[:, j, c0:c0 + W], in0=ps[:, b, 0:W], in1=cm[:, j, c0:c0 + W], op=ALU.mult),
                        reads=(("ps", b), ("cm", j, t)), writes=(("h", t),))
            w_out_phase(wB_out, lb, None)

        for g in range(n_groups):
            for t, (c0, W) in enumerate(TILES):
                for c in range(DC):
                    S.dma("sp", next_io(), lambda e, g=g, c=c, c0=c0, W=W: e.dma_start(
                        out=xT[:, c, c0:c0 + W], in_=xin[g][:, c, c0:c0 + W]), writes=(("x", t, c),))
            first_prenorm("g_mix_pre", 0)
            for l in range(n_layers):
                if l % 2 == 0:
                    layer_a(l, g)
                else:
                    layer_b(l, g)
                boundary("g_mix_post", l, "g_ffn_pre", l, g)
                ffn(l, g)
                if l + 1 < n_layers:
                    boundary("g_ffn_post", l, "g_mix_pre", l + 1, g)
                else:
                    boundary("g_ffn_post", l, None, None, g)
        for tok in out_toks:
            S.wait_tok("sp", tok)

        def run(eh, stream):
            for it in stream:
                if it[0] == "wait":
                    eh.wait_ge(sems[it[1]], it[2])
                else:
                    ins = it[1](eh)
                    if it[2] is not None:
                        ins.then_inc(sems[it[2]], it[3])

        with nc.Block() as block:
            @block.tensor
            def _(e):
                run(e, S.streams["pe"])

            @block.scalar
            def _(e):
                run(e, S.streams["act"])

            @block.vector
            def _(e):
                run(e, S.streams["dve"])

            @block.gpsimd
            def _(e):
                run(e, S.streams["pool"])

            @block.sync
            def _(e):
                run(e, S.streams["sp"])
    return nc


def _pack_vecs(inp):
    cols = []
    for name, shp in VEC_SPEC:
        a = np.asarray(inp[name], dtype=np.float32)
        if name == "a_w_dw":
            v = a.reshape(2, 31, 8, 128).transpose(3, 0, 2, 1)
        elif name == "b_w_conv":
            v = a.reshape(2, 3, 8, 128).transpose(3, 0, 2, 1)
        else:
            L = a.shape[0]
            v = a.reshape(L, -1, 128).transpose(2, 0, 1)
        cols.append(np.ascontiguousarray(v).reshape(128, -1))
    out = np.concatenate(cols, axis=1)
    assert out.shape == (128, NV)
    return np.ascontiguousarray(out)


def _pack_weights(inp):
    w = {}
    a = np.asarray(inp["a_w_in"], np.float32).reshape(2, 8, 128, 2, 8, 128)
    w["wA_in"] = np.ascontiguousarray(a.transpose(0, 4, 2, 1, 3, 5)).reshape(2, 8, 128, 2048)
    a = np.asarray(inp["a_w_out"], np.float32).reshape(2, 8, 128, 4, 256)
    w["wA_out"] = np.ascontiguousarray(a.transpose(0, 3, 2, 1, 4)).reshape(2, 4, 128, 2048)
    a = np.asarray(inp["b_w_in"], np.float32).reshape(2, 8, 128, 3, 8, 128)
    a = a[:, :, :, [1, 2, 0]]
    w["wB_in"] = np.ascontiguousarray(a.transpose(0, 4, 2, 1, 3, 5)).reshape(2, 8, 128, 3072)
    a = np.asarray(inp["b_w_out"], np.float32).reshape(2, 8, 128, 4, 256)
    w["wB_out"] = np.ascontiguousarray(a.transpose(0, 3, 2, 1, 4)).reshape(2, 4, 128, 2048)
    a = np.asarray(inp["w_gate_up"], np.float32).reshape(4, 8, 128, 2, FC, 128)
    w["wgu"] = np.ascontiguousarray(a.transpose(0, 4, 2, 1, 3, 5)).reshape(4, FC, 128, 2048)
    a = np.asarray(inp["w_down"], np.float32).reshape(4, FC, 128, 8, 128)
    w["wdn"] = np.ascontiguousarray(a.transpose(0, 3, 2, 1, 4)).reshape(4, 8, 128, FF)
    return w


def _pack_core_inputs(inp, i):
    xp = np.asarray(inp["x_prompt"], np.float32)[i]
    xs = np.asarray(inp["x_sample"], np.float32)[16 * i:16 * i + 16]
    xin = np.empty((NG, 128, DC, TG), np.float32)
    for g in range(NG):
        a = xp[1024 * g:1024 * g + 1024].reshape(1024, DC, 128)
        xin[g, :, :, 0:1024] = a.transpose(2, 1, 0)
        b = xs[8 * g:8 * g + 8].reshape(8, 8, DC, 128)
        xin[g, :, :, 1024:] = b.transpose(3, 2, 1, 0).reshape(128, DC, 64)
    sa_full = np.asarray(inp["state_conv_a"], np.float32)[:, 16 * i:16 * i + 16]
    sb_full = np.asarray(inp["state_conv_b"], np.float32)[:, 16 * i:16 * i + 16]
    sa = sa_full.reshape(2, NG, 8, 30, DC, 128).transpose(0, 1, 5, 4, 3, 2).reshape(2, NG, 128, DC, 240)
    sb = sb_full.reshape(2, NG, 8, 2, DC, 128).transpose(0, 1, 5, 4, 3, 2).reshape(2, NG, 128, DC, 16)
    return {"xin": np.ascontiguousarray(xin), "sa": np.ascontiguousarray(sa), "sb": np.ascontiguousarray(sb)}


_NC_CACHE = {}


def kernel(**inputs):
    shared = _pack_weights(inputs)
    shared["vecs"] = _pack_vecs(inputs)
    shared["ident"] = np.eye(128, dtype=np.float32)
    in_maps = []
    for i in range(NCORE):
        m = dict(shared)
        m.update(_pack_core_inputs(inputs, i))
        in_maps.append(m)
    if "nc" not in _NC_CACHE:
        _NC_CACHE["nc"] = build_program()
    nc = _NC_CACHE["nc"]
    res = run_bass_kernel_spmd(nc, in_maps, core_ids=list(range(NCORE)))
    outs = res.results

    y_prompt = np.empty((8, 2048, D), np.float32)
    y_sample = np.empty((128, 8, D), np.float32)
    nca_p = np.empty((2, 8, 30, D), np.float32)
    ncb_p = np.empty((2, 8, 2, D), np.float32)
    nca_s = np.empty((2, 128, 30, D), np.float32)
    ncb_s = np.empty((2, 128, 2, D), np.float32)
    for i in range(NCORE):
        r = outs[i]
        yT = np.asarray(r["yT"]).reshape(NG, 128, DC, TG)
        for g in range(NG):
            y_prompt[i, 1024 * g:1024 * g + 1024] = yT[g, :, :, 0:1024].transpose(2, 1, 0).reshape(1024, D)
            ys = yT[g, :, :, 1024:].reshape(128, DC, 8, 8)
            y_sample[16 * i + 8 * g:16 * i + 8 * g + 8] = ys.transpose(3, 2, 1, 0).reshape(8, 8, D)
        a = np.asarray(r["nca_p"]).reshape(2, 128, DC, 30)
        nca_p[:, i] = a.transpose(0, 3, 2, 1).reshape(2, 30, D)
        a = np.asarray(r["ncb_p"]).reshape(2, 128, DC, 2)
        ncb_p[:, i] = a.transpose(0, 3, 2, 1).reshape(2, 2, D)
        a = np.asarray(r["nca_s"]).reshape(2, NG, 128, DC, 30, 8)
        nca_s[:, 16 * i:16 * i + 16] = a.transpose(0, 1, 5, 4, 3, 2).reshape(2, 16, 30, D)
        a = np.asarray(r["ncb_s"]).reshape(2, NG, 128, DC, 2, 8)
        ncb_s[:, 16 * i:16 * i + 16] = a.transpose(0, 1, 5, 4, 3, 2).reshape(2, 16, 2, D)
    return (y_prompt, y_sample, nca_p, ncb_p, nca_s, ncb_s)
```

```python
import numpy as np
import concourse.bass as bass
import concourse.mybir as mybir
from concourse.bass_utils import run_bass_kernel_spmd

F32 = mybir.dt.float32
BF16 = mybir.dt.bfloat16
AF = mybir.ActivationFunctionType
ALU = mybir.AluOpType

D = 1024
DC = 8
FF = 2816
FC = 22
NCORE = 8
TG = 1088
NG = 2
TILES = [(0, 512), (512, 512), (1024, 64)]
KA, KB = 31, 3
RMS_EPS = 1e-6
LN_EPS = 1e-5
NSLOT = 6
SLOT_ELEMS = 3072

VEC_SPEC = [
    ("g_mix_pre", (4, 8)), ("g_mix_post", (4, 8)), ("g_ffn_pre", (4, 8)), ("g_ffn_post", (4, 8)),
    ("a_b_in", (2, 16)), ("a_w_dw", (2, 8, 31)), ("a_b_dw", (2, 8)), ("a_ln_g", (2, 8)),
    ("a_ln_b", (2, 8)), ("a_b_out", (2, 8)), ("b_w_conv", (2, 8, 3)),
]
VEC_OFF = {}
_o = 0
for _n, _s in VEC_SPEC:
    VEC_OFF[_n] = (_o, _s)
    _o += int(np.prod(_s))
NV = _o


class Sched:
    ENG = ("pe", "act", "dve", "pool", "sp")

    def __init__(self):
        self.streams = {e: [] for e in self.ENG}
        self.count = {}
        self.seen = {e: {} for e in self.ENG}
        self.last_w = {}
        self.readers = {}

    def _need(self, eng, tok, waits):
        sem, val, _ = tok
        if self.seen[eng].get(sem, 0) >= val:
            return
        if waits.get(sem, 0) < val:
            waits[sem] = val

    def _resolve(self, eng, reads, writes):
        waits = {}
        for k in reads:
            w = self.last_w.get(k)
            if w is not None:
                self._need(eng, w, waits)
        for k in writes:
            w = self.last_w.get(k)
            if w is not None and w[2] != eng:
                self._need(eng, w, waits)
            for r in self.readers.get(k, {}).values():
                if r[2] != eng:
                    self._need(eng, r, waits)
        for sem, val in waits.items():
            self.streams[eng].append(("wait", sem, val))
            self.seen[eng][sem] = val

    def _commit(self, tok, reads, writes):
        for k in reads:
            self.readers.setdefault(k, {})[tok[0]] = tok
        for k in writes:
            self.last_w[k] = tok
            self.readers[k] = {}

    def group(self, eng, fns, reads=(), writes=()):
        self._resolve(eng, reads, writes)
        self.count[eng] = self.count.get(eng, 0) + 1
        tok = (eng, self.count[eng], eng)
        for f in fns[:-1]:
            self.streams[eng].append(("op", f, None, 0))
        self.streams[eng].append(("op", fns[-1], eng, 1))
        self._commit(tok, reads, writes)
        return tok

    def op(self, eng, fn, reads=(), writes=()):
        return self.group(eng, [fn], reads, writes)

    def dma(self, q, sem, fn, reads=(), writes=()):
        self._resolve(q, reads, writes)
        prev = self.count.get(sem, 0)
        if prev and self.seen[q].get(sem, 0) < prev:
            self.streams[q].append(("wait", sem, prev))
            self.seen[q][sem] = prev
        self.count[sem] = prev + 16
        tok = (sem, prev + 16, None)
        self.streams[q].append(("op", fn, sem, 16))
        self._commit(tok, reads, writes)
        return tok

    def wait_tok(self, eng, tok):
        if self.seen[eng].get(tok[0], 0) < tok[1]:
            self.streams[eng].append(("wait", tok[0], tok[1]))
            self.seen[eng][tok[0]] = tok[1]


def build_program(n_layers=4, n_groups=NG):
    nc = bass.Bass("TRN2", target_bir_lowering=False)

    def din(name, shape):
        return nc.dram_tensor(name, list(shape), F32, kind="ExternalInput").ap()

    def dout(name, shape):
        return nc.dram_tensor(name, list(shape), F32, kind="ExternalOutput").ap()

    xin = din("xin", (NG, 128, DC, TG))
    sa = din("sa", (2, NG, 128, DC, 240))
    sb = din("sb", (2, NG, 128, DC, 16))
    vecs_d = din("vecs", (128, NV))
    ident_d = din("ident", (128, 128))
    wA_in = din("wA_in", (2, 8, 128, 2048))
    wA_out = din("wA_out", (2, 4, 128, 2048))
    wB_in = din("wB_in", (2, 8, 128, 3072))
    wB_out = din("wB_out", (2, 4, 128, 2048))
    wgu = din("wgu", (4, FC, 128, 2048))
    wdn = din("wdn", (4, 8, 128, FF))

    yT = dout("yT", (NG, 128, DC, TG))
    nca_p = dout("nca_p", (2, 128, DC, 30))
    ncb_p = dout("ncb_p", (2, 128, DC, 2))
    nca_s = dout("nca_s", (2, NG, 128, DC, 240))
    ncb_s = dout("ncb_s", (2, NG, 128, DC, 16))

    S = Sched()
    from contextlib import ExitStack
    es = ExitStack()

    def sb_t(name, shape, dt):
        return es.enter_context(nc.sbuf_tensor(name, list(shape), dt))

    with es:
        xT = sb_t("xT", (128, DC, TG), F32)
        hT = sb_t("hT", (128, DC, TG), BF16)
        cm = sb_t("cm", (128, DC, TG), F32)
        hid = sb_t("hid", (128, FC, TG), BF16)
        wsl = [sb_t(f"wsl{i}", (128, SLOT_ELEMS), BF16) for i in range(NSLOT)]
        sqb = [sb_t(f"sqb{i}", (128, DC, 512), BF16) for i in range(2)]
        sqb.append(sb_t("sqb2", (128, DC, 64), BF16))
        fsc = [sb_t(f"fsc{i}", (128, 512), F32)[:] for i in range(7)]
        fsc.append(fsc[4][:, 256:512])
        vecs = sb_t("vecs_sb", (128, NV), F32)
        ident_f = sb_t("ident_f", (128, 128), F32)
        ident_b = sb_t("ident_b", (128, 128), BF16)
        ones_b = sb_t("ones_b", (128, 128), BF16)
        eps_rms = sb_t("eps_rms", (128, 1), F32)
        eps_ln = sb_t("eps_ln", (128, 1), F32)
        dummy = sb_t("act_dummy", (128, 1), F32)
        vtail = sb_t("vtail", (128, DC, 30), F32)
        vnew_s = sb_t("vnew_s", (128, DC, 64), F32)
        ztail = sb_t("ztail", (128, DC, 2), F32)
        znew_s = sb_t("znew_s", (128, DC, 16), F32)
        histA = sb_t("histA", (128, 2, DC, 30), BF16)
        histB = sb_t("histB", (128, 2, DC, 2), BF16)
        ps = es.enter_context(nc.psum_tensor("ps", [128, 8, 512], F32))

        hid_flat = hid[:].rearrange("p a b -> p (a b)")
        VP_W = 1024 + 30
        VS_W = 240 + 64
        o0 = 0
        vT_p = hid_flat[:, o0:o0 + DC * VP_W].rearrange("p (c n) -> p c n", n=VP_W)
        o0 += DC * VP_W
        vT_s = hid_flat[:, o0:o0 + DC * VS_W].rearrange("p (c n) -> p c n", n=VS_W)
        o0 += DC * VS_W
        diag = []
        for i in range(3):
            diag.append(hid_flat[:, o0:o0 + KA * 128].rearrange("p (k n) -> p k n", n=128))
            o0 += KA * 128
        assert o0 <= FC * TG

        sem_names = list(Sched.ENG) + [f"w{i}" for i in range(NSLOT)] + [f"io{i}" for i in range(8)] + ["ph0", "ph1"]
        sems = {n: es.enter_context(nc.semaphore(n)) for n in sem_names}

        def vec(name, *idx):
            off, shp = VEC_OFF[name]
            strides = [int(np.prod(shp[i + 1:])) for i in range(len(shp))]
            o = off + sum(i * s for i, s in zip(idx, strides))
            n = strides[len(idx) - 1] if len(idx) else int(np.prod(shp))
            if len(idx) == len(shp):
                n = 1
            return vecs[:, o:o + n]

        st = {"bank": 0, "piece": 0, "io": 0, "fs": 0, "sq": 0, "dg": 0}

        def next_bank():
            b = st["bank"]
            st["bank"] = (b + 1) % 8
            return b

        def next_io():
            i = st["io"]
            st["io"] = (i + 1) % 8
            return f"io{i}"

        def next_fs(lo=0, hi=2):
            k = ("fsrot", lo, hi)
            i = st.get(k, lo)
            st[k] = lo + ((i - lo + 1) % (hi - lo))
            return i

        def next_sq():
            i = st["sq"]
            st["sq"] = (i + 1) % 2
            return i

        out_toks = []

        def load_piece(dram_ap, n):
            s = st["piece"] % NSLOT
            st["piece"] += 1
            dst = wsl[s][:, 0:n]
            S.dma("pool", f"w{s}", lambda e, d=dst, a=dram_ap: e.dma_start(out=d, in_=a),
                  reads=(), writes=(("w", s),))
            return s

        def mm_group(psap, pairs, reads, writes):
            n = len(pairs)
            fns = []
            for i, (l, r) in enumerate(pairs):
                fns.append(lambda e, l=l, r=r, i=i: e.matmul(psap, l, r, start=(i == 0), stop=(i == n - 1)))
            return S.group("pe", fns, reads, writes)

        S.dma("sp", next_io(), lambda e: e.dma_start(out=vecs[:], in_=vecs_d), writes=(("vec",),))
        S.dma("sp", next_io(), lambda e: e.dma_start(out=ident_f[:], in_=ident_d), writes=(("identf",),))
        S.op("dve", lambda e: e.tensor_copy(out=ident_b[:], in_=ident_f[:]), reads=(("identf",),), writes=(("ident",),))
        S.op("dve", lambda e: e.memset(ones_b[:], 1.0), writes=(("ones",),))
        S.op("dve", lambda e: e.memset(eps_rms[:], float(D * RMS_EPS)), writes=(("eps",),))
        S.op("dve", lambda e: e.memset(eps_ln[:], float(LN_EPS)), writes=(("eps",),))
        S.op("dve", lambda e: e.tensor_scalar(out=vecs[:, 0:128], in0=vecs[:, 0:128], scalar1=float(np.sqrt(D)),
                                              scalar2=None, op0=ALU.mult),
             reads=(("vec",),), writes=(("vec",),))

        NT = len(TILES)
        def rms_stats(src_key_reads, sq_i, W):
            b = next_bank()
            pairs = [(ones_b[:], sqb[sq_i][:, c, 0:W]) for c in range(DC)]
            mm_group(ps[:, b, 0:W], pairs, reads=(("sq", sq_i), ("ones",)), writes=(("ps", b),))
            return b

        def stat_sqrt(src_sq_fn, src_reads, t, fi, eng="act", defer=False):
            c0, W = TILES[t]
            qi = next_sq()
            S.op(eng, lambda e: src_sq_fn(e, sqb[qi][:, :, 0:W]), reads=src_reads, writes=(("sq", qi),))
            b = rms_stats(None, qi, W)

            def fin():
                S.op("act", lambda e: e.activation(out=fsc[fi][:, 0:W], in_=ps[:, b, 0:W], func=AF.Sqrt, bias=eps_rms[:, 0:1]),
                     reads=(("ps", b), ("eps",)), writes=(("f", fi),))
            if defer:
                return fin
            fin()
            return None

        def recip(t, fi):
            c0, W = TILES[t]
            S.op("dve", lambda e: e.reciprocal(out=fsc[fi][:, 0:W], in_=fsc[fi][:, 0:W]),
                 reads=(("f", fi),), writes=(("f", fi),))

        def xk(t):
            return tuple(("x", t, c) for c in range(DC))

        def stage_x_stats(t, fi, defer=False):
            c0, W = TILES[t]
            return stat_sqrt(lambda e, o: e.activation(out=o, in_=xT[:, :, c0:c0 + W], func=AF.Square), xk(t), t, fi,
                             defer=defer)

        def cm_square(t):
            c0, W = TILES[t]
            cmkeys = tuple(("cm", d, t) for d in range(DC))
            if t == 1:
                S.op("dve", lambda e: e.tensor_tensor(out=sqb[t][:, :, 0:W], in0=cm[:, :, c0:c0 + W],
                                                      in1=cm[:, :, c0:c0 + W], op=ALU.mult),
                     reads=cmkeys, writes=(("sq", t),))
            else:
                S.op("act", lambda e: e.activation(out=sqb[t][:, :, 0:W], in_=cm[:, :, c0:c0 + W], func=AF.Square),
                     reads=cmkeys, writes=(("sq", t),))

        def cm_stats(t, fi):
            c0, W = TILES[t]
            b = rms_stats(None, t, W)
            S.op("act", lambda e: e.activation(out=fsc[fi][:, 0:W], in_=ps[:, b, 0:W], func=AF.Sqrt, bias=eps_rms[:, 0:1]),
                 reads=(("ps", b), ("eps",)), writes=(("f", fi),))

        def stage_h(gname, l, t):
            c0, W = TILES[t]
            for c in range(DC):
                S.op("act", lambda e, c=c: e.activation(out=hT[:, c, c0:c0 + W], in_=xT[:, c, c0:c0 + W],
                                                        func=AF.Copy, scale=vec(gname, l, c)),
                     reads=(("x", t, c), ("vec",)), writes=(("h", t),))

        def stage_resid(gname, l, t, fi, pre_g=None, l2=None):
            c0, W = TILES[t]

            def stt(c):
                S.op("dve", lambda e: e.scalar_tensor_tensor(
                    out=cm[:, c, c0:c0 + W], in0=cm[:, c, c0:c0 + W], scalar=vec(gname, l, c),
                    in1=fsc[fi][:, 0:W], op0=ALU.mult, op1=ALU.mult),
                    reads=(("cm", c, t), ("f", fi), ("vec",)), writes=(("cm", c, t),))

            def add_h(c):
                S.op("dve", lambda e: e.tensor_tensor(out=xT[:, c, c0:c0 + W], in0=xT[:, c, c0:c0 + W],
                                                      in1=cm[:, c, c0:c0 + W], op=ALU.add),
                     reads=(("cm", c, t), ("x", t, c)), writes=(("x", t, c),))
                if pre_g is not None:
                    S.op("act", lambda e: e.activation(out=hT[:, c, c0:c0 + W], in_=xT[:, c, c0:c0 + W],
                                                       func=AF.Copy, scale=vec(pre_g, l2, c)),
                         reads=(("x", t, c), ("vec",)), writes=(("h", t),))

            for c in range(DC):
                stt(c)
                if c >= 1:
                    add_h(c - 1)
            add_h(DC - 1)

        R1 = [2, 3, 4]
        R2 = [5, 6, 7]

        def rb(t):
            c0, W = TILES[t]
            return fsc[R2[t]][:, 0:W]

        pend_x = {}

        def x_square(t):
            c0, W = TILES[t]
            S.op("act", lambda e: e.activation(out=sqb[t][:, :, 0:W], in_=xT[:, :, c0:c0 + W], func=AF.Square),
                 reads=xk(t), writes=(("sq", t),))
            pend_x[t] = True

        def need_r(t):
            if pend_x.get(t):
                pend_x[t] = False
                c0, W = TILES[t]
                b = rms_stats(None, t, W)
                fi = R2[t]
                S.op("act", lambda e: e.activation(out=fsc[fi][:, 0:W], in_=ps[:, b, 0:W], func=AF.Sqrt, bias=eps_rms[:, 0:1]),
                     reads=(("ps", b), ("eps",)), writes=(("f", fi),))
                recip(t, fi)

        def rkey(t):
            return ("f", R2[t])

        def load_x(g, t):
            c0, W = TILES[t]
            for c in range(DC):
                S.dma("sp", next_io(), lambda e, c=c: e.dma_start(
                    out=xT[:, c, c0:c0 + W], in_=xin[g][:, c, c0:c0 + W]), writes=(("x", t, c),))

        def first_prenorm(gname, l):
            for t in range(NT):
                stage_h(gname, l, t)
                x_square(t)

        def boundary(post_g, l, pre_g, l2, g):
            for t in range(NT):
                if t == 0:
                    S.op("act", lambda e: e.activation(out=dummy[:, 0:1], in_=eps_ln[:, 0:1], func=AF.Sqrt),
                         reads=(("eps",),), writes=(("dummy",),))
                cm_stats(t, R1[t])
            for t in range(NT):
                recip(t, R1[t])
                stage_resid(post_g, l, t, R1[t], pre_g, l2)
                if pre_g is not None:
                    x_square(t)
                else:
                    c0, W = TILES[t]
                    for c in range(DC):
                        out_toks.append(S.dma("sp", next_io(), lambda e, c=c, c0=c0, W=W: e.dma_start(
                            out=yT[g][:, c, c0:c0 + W], in_=xT[:, c, c0:c0 + W]), reads=(("x", t, c),)))
                    if g + 1 < n_groups:
                        load_x(g + 1, t)

        def build_diag(wname, l, j, K):
            di = st["dg"]
            st["dg"] = 1 - di
            dst = diag[di][:, 0:K, :]
            w_ap = vec(wname, l, j)
            in0 = ident_b[:].unsqueeze(1).broadcast_to([128, K, 128])
            in1 = w_ap.unsqueeze(2).broadcast_to([128, K, 128])
            S.op("pool", lambda e: e.tensor_tensor(out=dst, in0=in0, in1=in1, op=ALU.mult),
                 reads=(("ident",), ("vec",)), writes=(("diag", di), ("hidalias",)))
            return di

        def conv_rhs(j, t, k, hist):
            c0, W = TILES[t]
            if t < 2:
                return vT_p[:, j, c0 + k:c0 + k + W]
            return vT_s[:, j, 8 * k:8 * k + 64]

        def conv_reads(j, t):
            if t == 0:
                return (("v", j, "hist"), ("v", j, 0))
            if t == 1:
                return (("v", j, 0), ("v", j, 1))
            return (("v", j, "shist"), ("v", j, 2))

        def vdst(j, t, hist):
            c0, W = TILES[t]
            if t < 2:
                return vT_p[:, j, hist + c0:hist + c0 + W]
            return vT_s[:, j, 8 * hist:8 * hist + 64]

        def init_hist(kind, lm, g, hist, hbuf, s_in):
            keys = tuple(("v", j, "hist") for j in range(DC))
            if g == 0:
                S.op("dve", lambda e: e.memset(vT_p[:, :, 0:hist], 0.0), writes=keys + (("hidalias",),))
            else:
                S.op("dve", lambda e: e.tensor_copy(out=vT_p[:, :, 0:hist], in_=hbuf[:, lm, :, :]),
                     reads=(("hist", kind, lm),), writes=keys + (("hidalias",),))

        def init_hist_pool(lm, g, hist, s_in):
            skeys = tuple(("v", j, "shist") for j in range(DC))
            st["ph"] = 1 - st.get("ph", 0)
            S.dma("pool", f"ph{st['ph']}", lambda e: e.dma_start(out=vT_s[:, :, 0:8 * hist], in_=s_in[lm, g]),
                  writes=skeys + (("hidalias",),))

        BS = 3

        def blocks(n, bs=BS):
            return [list(range(b0, min(b0 + bs, n))) for b0 in range(0, n, bs)]

        def hmm(psb, W, w3, lo, hi, c0, s, t):
            mm_group(ps[:, psb, 0:W], [(w3[:, kc, lo:hi], hT[:, kc, c0:c0 + W]) for kc in range(DC)],
                     reads=(("w", s), ("h", t)), writes=(("ps", psb),))

        def ffn(l, g):
            for blk in blocks(FC):
                slots = {j: load_piece(wgu[l, j], 2048) for j in blk}
                for t, (c0, W) in enumerate(TILES):
                    for j in blk:
                        s = slots[j]
                        w3 = wsl[s][:, 0:2048].rearrange("p (k n) -> p k n", n=256)
                        bg = next_bank()
                        hmm(bg, W, w3, 0, 128, c0, s, t)
                        bu = next_bank()
                        hmm(bu, W, w3, 128, 256, c0, s, t)
                        need_r(t)
                        f0 = next_fs(0, 4)
                        f1 = next_fs(0, 4)
                        S.op("dve", lambda e, bg=bg, f0=f0, W=W, t=t: e.tensor_tensor(
                            out=fsc[f0][:, 0:W], in0=ps[:, bg, 0:W], in1=rb(t), op=ALU.mult),
                            reads=(("ps", bg), rkey(t)), writes=(("f", f0),))
                        S.op("dve", lambda e, bu=bu, f1=f1, W=W, t=t: e.tensor_tensor(
                            out=fsc[f1][:, 0:W], in0=ps[:, bu, 0:W], in1=rb(t), op=ALU.mult),
                            reads=(("ps", bu), rkey(t)), writes=(("f", f1),))
                        S.op("act", lambda e, f0=f0, W=W: e.activation(out=fsc[f0][:, 0:W], in_=fsc[f0][:, 0:W], func=AF.Silu),
                             reads=(("f", f0),), writes=(("f", f0),))
                        S.op("dve", lambda e, f0=f0, f1=f1, W=W, j=j, c0=c0: e.tensor_tensor(
                            out=hid[:, j, c0:c0 + W], in0=fsc[f1][:, 0:W], in1=fsc[f0][:, 0:W], op=ALU.mult),
                            reads=(("f", f0), ("f", f1)), writes=(("hid", j, t),))
            for blk in blocks(DC):
                slots = {d: load_piece(wdn[l, d], FF) for d in blk}
                for t, (c0, W) in enumerate(TILES):
                    for d in blk:
                        s = slots[d]
                        w3 = wsl[s][:, 0:FF].rearrange("p (k n) -> p k n", n=128)
                        b = next_bank()
                        mm_group(ps[:, b, 0:W], [(w3[:, kc, :], hid[:, kc, c0:c0 + W]) for kc in range(FC)],
                                 reads=(("w", s), ("hidalias",)) + tuple(("hid", kc, t) for kc in range(FC)),
                                 writes=(("ps", b),))
                        S.op("act", lambda e, b=b, d=d, c0=c0, W=W: e.activation(out=cm[:, d, c0:c0 + W], in_=ps[:, b, 0:W], func=AF.Copy),
                             reads=(("ps", b),), writes=(("cm", d, t),))
                    if blk[-1] == DC - 1:
                        cm_square(t)

        def w_out_phase(wdram, lm, bias_name):
            blk = list(range(4))
            slots = {p: load_piece(wdram[lm, p], 2048) for p in blk}
            for t, (c0, W) in enumerate(TILES):
                for p in blk:
                    s = slots[p]
                    w3 = wsl[s][:, 0:2048].rearrange("p (k n) -> p k n", n=256)
                    for dd in range(2):
                        d = 2 * p + dd
                        b = next_bank()
                        hmm(b, W, w3, dd * 128, (dd + 1) * 128, c0, s, t)
                        if bias_name is not None:
                            S.op("act", lambda e, b=b, d=d, c0=c0, W=W: e.activation(
                                out=cm[:, d, c0:c0 + W], in_=ps[:, b, 0:W], func=AF.Identity, bias=vec(bias_name, lm, d)),
                                reads=(("ps", b), ("vec",)), writes=(("cm", d, t),))
                        else:
                            S.op("act", lambda e, b=b, d=d, c0=c0, W=W: e.activation(out=cm[:, d, c0:c0 + W], in_=ps[:, b, 0:W], func=AF.Copy),
                                 reads=(("ps", b),), writes=(("cm", d, t),))
                cm_square(t)

        def layer_a(l, g):
            la = l // 2
            init_hist("a", la, g, 30, histA, sa)

            def build_a(j, di):
                dst = diag[di][:, 0:KA, :]
                in0 = ident_b[:].unsqueeze(1).broadcast_to([128, KA, 128])
                in1 = vec("a_w_dw", la, j).unsqueeze(2).broadcast_to([128, KA, 128])
                S.op("dve", lambda e: e.tensor_tensor(out=dst, in0=in0, in1=in1, op=ALU.mult),
                     reads=(("ident",), ("vec",)), writes=(("diag", di), ("hidalias",)))

            def conv_block(blk):
                for t, (c0, W) in enumerate(TILES):
                    for j in blk:
                        di = j % 3
                        b = next_bank()
                        mm_group(ps[:, b, 0:W], [(diag[di][:, k, :], conv_rhs(j, t, k, 30)) for k in range(KA)],
                                 reads=(("diag", di),) + conv_reads(j, t), writes=(("ps", b),))
                        S.op("act", lambda e, b=b, j=j, c0=c0, W=W: e.activation(
                            out=cm[:, j, c0:c0 + W], in_=ps[:, b, 0:W], func=AF.Identity, bias=vec("a_b_dw", la, j)),
                            reads=(("ps", b), ("vec",)), writes=(("cm", j, t),))

            def win_block(blk):
                slots = {j: load_piece(wA_in[la, j], 2048) for j in blk}
                for t, (c0, W) in enumerate(TILES):
                    for j in blk:
                        s = slots[j]
                        w3 = wsl[s][:, 0:2048].rearrange("p (k n) -> p k n", n=256)
                        ba = next_bank()
                        hmm(ba, W, w3, 0, 128, c0, s, t)
                        bgk = next_bank()
                        hmm(bgk, W, w3, 128, 256, c0, s, t)
                        need_r(t)
                        fi = next_fs(0, 4)
                        f1 = next_fs(0, 4)
                        S.op("dve", lambda e, bgk=bgk, fi=fi, W=W, t=t: e.tensor_tensor(
                            out=fsc[fi][:, 0:W], in0=ps[:, bgk, 0:W], in1=rb(t), op=ALU.mult),
                            reads=(("ps", bgk), rkey(t)), writes=(("f", fi),))
                        S.op("dve", lambda e, ba=ba, f1=f1, W=W, t=t: e.tensor_tensor(
                            out=fsc[f1][:, 0:W], in0=ps[:, ba, 0:W], in1=rb(t), op=ALU.mult),
                            reads=(("ps", ba), rkey(t)), writes=(("f", f1),))
                        S.op("act", lambda e, fi=fi, W=W, j=j: e.activation(
                            out=fsc[fi][:, 0:W], in_=fsc[fi][:, 0:W], func=AF.Sigmoid, bias=vec("a_b_in", la, 8 + j)),
                            reads=(("f", fi), ("vec",)), writes=(("f", fi),))
                        fns = [lambda e, f1=f1, fi=fi, W=W, j=j, t=t: e.scalar_tensor_tensor(
                            out=vdst(j, t, 30), in0=fsc[f1][:, 0:W], scalar=vec("a_b_in", la, j),
                            in1=fsc[fi][:, 0:W], op0=ALU.add, op1=ALU.mult)]
                        wr = [("v", j, t)]
                        if t == 1 and g == 1:
                            fns.append(lambda e, f1=f1, fi=fi, j=j: e.scalar_tensor_tensor(
                                out=vtail[:, j, :], in0=fsc[f1][:, 482:512], scalar=vec("a_b_in", la, j),
                                in1=fsc[fi][:, 482:512], op0=ALU.add, op1=ALU.mult))
                            wr.append(("vtail", j))
                        if t == 2:
                            fns.append(lambda e, f1=f1, fi=fi, j=j: e.scalar_tensor_tensor(
                                out=vnew_s[:, j, :], in0=fsc[f1][:, 0:64], scalar=vec("a_b_in", la, j),
                                in1=fsc[fi][:, 0:64], op0=ALU.add, op1=ALU.mult))
                            wr.append(("vnew", j))
                        S.group("dve", fns, reads=(("f", f1), ("f", fi), ("vec",)), writes=tuple(wr))
                        if t == 1 and g == 0:
                            S.op("dve", lambda e, j=j: e.tensor_copy(out=histA[:, la, j, :], in_=vT_p[:, j, 1024:1054]),
                                 reads=(("v", j, 1),), writes=(("hist", "a", la),))

            blks = blocks(DC)
            for j in blks[0]:
                build_a(j, j % 3)
            win_block(blks[0])
            init_hist_pool(la, g, 30, sa)
            for bi in range(1, len(blks)):
                win_block(blks[bi])
                conv_block(blks[bi - 1])
                for j in blks[bi]:
                    build_a(j, j % 3)
            conv_block(blks[-1])
            if g == 1:
                out_toks.append(S.dma("sp", next_io(), lambda e: e.dma_start(out=nca_p[la], in_=vtail[:]),
                                      reads=tuple(("vtail", j) for j in range(DC))))
            out_toks.append(S.dma("sp", next_io(), lambda e: e.dma_start(out=nca_s[la, g][:, :, 176:240], in_=vnew_s[:]),
                                  reads=tuple(("vnew", j) for j in range(DC))))
            out_toks.append(S.dma("sp", next_io(), lambda e: e.dma_start(out=nca_s[la, g][:, :, 0:176],
                                                                         in_=sa[la, g][:, :, 64:240])))
            MU, RS = R1, R2
            lnb = {}
            for t, (c0, W) in enumerate(TILES):
                cmkeys = tuple(("cm", j, t) for j in range(DC))
                q1 = next_sq()
                S.op("dve", lambda e, q1=q1, c0=c0, W=W: e.tensor_copy(out=sqb[q1][:, :, 0:W], in_=cm[:, :, c0:c0 + W]),
                     reads=cmkeys, writes=(("sq", q1),))
                b1 = rms_stats(None, q1, W)
                q2 = next_sq()
                S.op("act", lambda e, q2=q2, c0=c0, W=W: e.activation(out=sqb[q2][:, :, 0:W], in_=cm[:, :, c0:c0 + W], func=AF.Square),
                     reads=cmkeys, writes=(("sq", q2),))
                b2 = rms_stats(None, q2, W)
                lnb[t] = (b1, b2)
                fmu, frs = MU[t], RS[t]
                S.op("dve", lambda e, b1=b1, W=W, fmu=fmu: e.tensor_scalar(out=fsc[fmu][:, 0:W], in0=ps[:, b1, 0:W], scalar1=1.0 / D,
                                                                          scalar2=None, op0=ALU.mult),
                     reads=(("ps", b1),), writes=(("f", fmu),))
                S.op("dve", lambda e, W=W, fmu=fmu, frs=frs: e.tensor_tensor(out=fsc[frs][:, 0:W], in0=fsc[fmu][:, 0:W], in1=fsc[fmu][:, 0:W], op=ALU.mult),
                     reads=(("f", fmu),), writes=(("f", frs),))
                S.op("dve", lambda e, b2=b2, W=W, frs=frs: e.scalar_tensor_tensor(out=fsc[frs][:, 0:W], in0=ps[:, b2, 0:W], scalar=1.0 / D,
                                                                                 in1=fsc[frs][:, 0:W], op0=ALU.mult, op1=ALU.subtract),
                     reads=(("ps", b2), ("f", frs)), writes=(("f", frs),))
                S.op("act", lambda e, W=W, frs=frs: e.activation(out=fsc[frs][:, 0:W], in_=fsc[frs][:, 0:W], func=AF.Sqrt, bias=eps_ln[:, 0:1]),
                     reads=(("f", frs), ("eps",)), writes=(("f", frs),))
            for t, (c0, W) in enumerate(TILES):
                cmkeys = tuple(("cm", j, t) for j in range(DC))
                fmu, frs = MU[t], RS[t]
                S.op("dve", lambda e, W=W, frs=frs: e.reciprocal(out=fsc[frs][:, 0:W], in_=fsc[frs][:, 0:W]),
                     reads=(("f", frs),), writes=(("f", frs),))
                def ln_sub(j, c0=c0, W=W, fmu=fmu, t=t):
                    S.op("dve", lambda e: e.tensor_tensor(out=cm[:, j, c0:c0 + W], in0=cm[:, j, c0:c0 + W],
                                                          in1=fsc[fmu][:, 0:W], op=ALU.subtract),
                         reads=(("cm", j, t), ("f", fmu)), writes=(("cm", j, t),))

                def ln_mul_silu(j, c0=c0, W=W, frs=frs, t=t):
                    S.op("dve", lambda e: e.tensor_tensor(out=cm[:, j, c0:c0 + W], in0=cm[:, j, c0:c0 + W],
                                                          in1=fsc[frs][:, 0:W], op=ALU.mult),
                         reads=(("cm", j, t), ("f", frs)), writes=(("cm", j, t),))
                    S.op("act", lambda e: e.activation(
                        out=hT[:, j, c0:c0 + W], in_=cm[:, j, c0:c0 + W], func=AF.Silu,
                        bias=vec("a_ln_b", la, j), scale=vec("a_ln_g", la, j)),
                        reads=(("cm", j, t), ("vec",)), writes=(("h", t),))

                for j in range(DC):
                    ln_sub(j)
                    if j >= 1:
                        ln_mul_silu(j - 1)
                ln_mul_silu(DC - 1)
            w_out_phase(wA_out, la, "a_b_out")

        def layer_b(l, g):
            lb = l // 2
            init_hist("b", lb, g, 2, histB, sb)
            dB = diag[0][:, 0:DC * KB, :]
            in0 = ident_b[:].unsqueeze(1).broadcast_to([128, DC * KB, 128])
            off, _ = VEC_OFF["b_w_conv"]
            wv = vecs[:, off + lb * DC * KB: off + (lb + 1) * DC * KB]
            in1 = wv.unsqueeze(2).broadcast_to([128, DC * KB, 128])
            r2done = {}
            for bi_, blk in enumerate(blocks(DC)):
                if bi_ == 1:
                    init_hist_pool(lb, g, 2, sb)
                    S.op("pool", lambda e: e.tensor_tensor(out=dB, in0=in0, in1=in1, op=ALU.mult),
                         reads=(("ident",), ("vec",)), writes=(("diag", 0), ("hidalias",)))
                slots = {j: load_piece(wB_in[lb, j], 3072) for j in blk}
                for t, (c0, W) in enumerate(TILES):
                    for j in blk:
                        s = slots[j]
                        w3 = wsl[s][:, 0:3072].rearrange("p (k n) -> p k n", n=384)
                        bc = next_bank()
                        hmm(bc, W, w3, 0, 128, c0, s, t)
                        bv = next_bank()
                        hmm(bv, W, w3, 128, 256, c0, s, t)
                        bb = next_bank()
                        hmm(bb, W, w3, 256, 384, c0, s, t)
                        need_r(t)
                        if not r2done.get(t):
                            r2done[t] = True
                            S.op("dve", lambda e, t=t, W=W: e.tensor_tensor(out=fsc[R1[t]][:, 0:W], in0=rb(t), in1=rb(t), op=ALU.mult),
                                 reads=(rkey(t),), writes=(("f", R1[t]),))
                        fi = next_fs(0, 2)
                        f1 = next_fs(0, 2)
                        S.op("act", lambda e, bc=bc, fi=fi, W=W: e.activation(out=fsc[fi][:, 0:W], in_=ps[:, bc, 0:W], func=AF.Copy),
                             reads=(("ps", bc),), writes=(("f", fi),))
                        S.op("dve", lambda e, bb=bb, j=j, c0=c0, W=W, t=t: e.tensor_tensor(
                            out=cm[:, j, c0:c0 + W], in0=ps[:, bb, 0:W], in1=rb(t), op=ALU.mult),
                            reads=(("ps", bb), rkey(t)), writes=(("cm", j, t),))
                        S.op("dve", lambda e, bv=bv, fi=fi, f1=f1, W=W: e.tensor_tensor(
                            out=fsc[f1][:, 0:W], in0=ps[:, bv, 0:W], in1=fsc[fi][:, 0:W], op=ALU.mult),
                            reads=(("ps", bv), ("f", fi)), writes=(("f", f1),))
                        r2 = fsc[R1[t]]
                        fns = [lambda e, f1=f1, W=W, j=j, t=t, r2=r2: e.tensor_tensor(
                            out=vdst(j, t, 2), in0=fsc[f1][:, 0:W], in1=r2[:, 0:W], op=ALU.mult)]
                        wr = [("v", j, t)]
                        if t == 1 and g == 1:
                            fns.append(lambda e, f1=f1, j=j, r2=r2: e.tensor_tensor(
                                out=ztail[:, j, :], in0=fsc[f1][:, 510:512], in1=r2[:, 510:512], op=ALU.mult))
                            wr.append(("ztail", j))
                        if t == 2:
                            fns.append(lambda e, f1=f1, j=j, r2=r2: e.tensor_tensor(
                                out=znew_s[:, j, :], in0=fsc[f1][:, 48:64], in1=r2[:, 48:64], op=ALU.mult))
                            wr.append(("znew", j))
                        S.group("dve", fns, reads=(("f", f1), ("f", R1[t])), writes=tuple(wr))
                        if t == 1 and g == 0:
                            S.op("dve", lambda e, j=j: e.tensor_copy(out=histB[:, lb, j, :], in_=vT_p[:, j, 1024:1026]),
                                 reads=(("v", j, 1),), writes=(("hist", "b", lb),))
            if g == 1:
                out_toks.append(S.dma("sp", next_io(), lambda e: e.dma_start(out=ncb_p[lb], in_=ztail[:]),
                                      reads=tuple(("ztail", j) for j in range(DC))))
            out_toks.append(S.dma("sp", next_io(), lambda e: e.dma_start(out=ncb_s[lb, g], in_=znew_s[:]),
                                  reads=tuple(("znew", j) for j in range(DC))))
            for t, (c0, W) in enumerate(TILES):
                for j in range(DC):
                    b = next_bank()
                    mm_group(ps[:, b, 0:W], [(dB[:, j * KB + k, :], conv_rhs(j, t, k, 2)) for k in range(KB)],
                             reads=(("diag", 0),) + conv_reads(j, t), writes=(("ps", b),))
                    S.op("dve", lambda e, b=b, j=j, c0=c0, W=W: e.tensor_tensor(
                        out=hT[:, j, c0:c0 + W], in0=ps[:, b, 0:W], in1=cm[:, j, c0:c0 + W], op=ALU.mult),
                        reads=(("ps", b), ("cm", j, t)), writes=(("h", t),))
            w_out_phase(wB_out, lb, None)

        for g in range(n_groups):
            if g == 0:
                for t in range(NT):
                    load_x(0, t)
            first_prenorm("g_mix_pre", 0)
            for l in range(n_layers):
                if l % 2 == 0:
                    layer_a(l, g)
                else:
                    layer_b(l, g)
                boundary("g_mix_post", l, "g_ffn_pre", l, g)
                ffn(l, g)
                if l + 1 < n_layers:
                    boundary("g_ffn_post", l, "g_mix_pre", l + 1, g)
                else:
                    boundary("g_ffn_post", l, None, None, g)
        for tok in out_toks:
            S.wait_tok("sp", tok)

        def run(eh, stream):
            for it in stream:
                if it[0] == "wait":
                    eh.wait_ge(sems[it[1]], it[2])
                else:
                    ins = it[1](eh)
                    if it[2] is not None:
                        ins.then_inc(sems[it[2]], it[3])

        with nc.Block() as block:
            @block.tensor
            def _(e):
                run(e, S.streams["pe"])

            @block.scalar
            def _(e):
                run(e, S.streams["act"])

            @block.vector
            def _(e):
                run(e, S.streams["dve"])

            @block.gpsimd
            def _(e):
                run(e, S.streams["pool"])

            @block.sync
            def _(e):
                run(e, S.streams["sp"])
    return nc


def _pack_vecs(inp):
    cols = []
    for name, shp in VEC_SPEC:
        a = np.asarray(inp[name], dtype=np.float32)
        if name == "a_w_dw":
            v = a.reshape(2, 31, 8, 128).transpose(3, 0, 2, 1)
        elif name == "b_w_conv":
            v = a.reshape(2, 3, 8, 128).transpose(3, 0, 2, 1)
        else:
            L = a.shape[0]
            v = a.reshape(L, -1, 128).transpose(2, 0, 1)
        cols.append(np.ascontiguousarray(v).reshape(128, -1))
    out = np.concatenate(cols, axis=1)
    assert out.shape == (128, NV)
    return np.ascontiguousarray(out)


def _pack_weights(inp):
    w = {}
    a = np.asarray(inp["a_w_in"], np.float32).reshape(2, 8, 128, 2, 8, 128)
    w["wA_in"] = np.ascontiguousarray(a.transpose(0, 4, 2, 1, 3, 5)).reshape(2, 8, 128, 2048)
    a = np.asarray(inp["a_w_out"], np.float32).reshape(2, 8, 128, 4, 256)
    w["wA_out"] = np.ascontiguousarray(a.transpose(0, 3, 2, 1, 4)).reshape(2, 4, 128, 2048)
    a = np.asarray(inp["b_w_in"], np.float32).reshape(2, 8, 128, 3, 8, 128)
    a = a[:, :, :, [1, 2, 0]]
    w["wB_in"] = np.ascontiguousarray(a.transpose(0, 4, 2, 1, 3, 5)).reshape(2, 8, 128, 3072)
    a = np.asarray(inp["b_w_out"], np.float32).reshape(2, 8, 128, 4, 256)
    w["wB_out"] = np.ascontiguousarray(a.transpose(0, 3, 2, 1, 4)).reshape(2, 4, 128, 2048)
    a = np.asarray(inp["w_gate_up"], np.float32).reshape(4, 8, 128, 2, FC, 128)
    w["wgu"] = np.ascontiguousarray(a.transpose(0, 4, 2, 1, 3, 5)).reshape(4, FC, 128, 2048)
    a = np.asarray(inp["w_down"], np.float32).reshape(4, FC, 128, 8, 128)
    w["wdn"] = np.ascontiguousarray(a.transpose(0, 3, 2, 1, 4)).reshape(4, 8, 128, FF)
    return w


def _pack_core_inputs(inp, i):
    xp = np.asarray(inp["x_prompt"], np.float32)[i]
    xs = np.asarray(inp["x_sample"], np.float32)[16 * i:16 * i + 16]
    xin = np.empty((NG, 128, DC, TG), np.float32)
    for g in range(NG):
        a = xp[1024 * g:1024 * g + 1024].reshape(1024, DC, 128)
        xin[g, :, :, 0:1024] = a.transpose(2, 1, 0)
        b = xs[8 * g:8 * g + 8].reshape(8, 8, DC, 128)
        xin[g, :, :, 1024:] = b.transpose(3, 2, 1, 0).reshape(128, DC, 64)
    sa_full = np.asarray(inp["state_conv_a"], np.float32)[:, 16 * i:16 * i + 16]
    sb_full = np.asarray(inp["state_conv_b"], np.float32)[:, 16 * i:16 * i + 16]
    sa = sa_full.reshape(2, NG, 8, 30, DC, 128).transpose(0, 1, 5, 4, 3, 2).reshape(2, NG, 128, DC, 240)
    sb = sb_full.reshape(2, NG, 8, 2, DC, 128).transpose(0, 1, 5, 4, 3, 2).reshape(2, NG, 128, DC, 16)
    return {"xin": np.ascontiguousarray(xin), "sa": np.ascontiguousarray(sa), "sb": np.ascontiguousarray(sb)}


_NC_CACHE = {}


def kernel(**inputs):
    shared = _pack_weights(inputs)
    shared["vecs"] = _pack_vecs(inputs)
    shared["ident"] = np.eye(128, dtype=np.float32)
    in_maps = []
    for i in range(NCORE):
        m = dict(shared)
        m.update(_pack_core_inputs(inputs, i))
        in_maps.append(m)
    if "nc" not in _NC_CACHE:
        _NC_CACHE["nc"] = build_program()
    nc = _NC_CACHE["nc"]
    res = run_bass_kernel_spmd(nc, in_maps, core_ids=list(range(NCORE)))
    outs = res.results

    y_prompt = np.empty((8, 2048, D), np.float32)
    y_sample = np.empty((128, 8, D), np.float32)
    nca_p = np.empty((2, 8, 30, D), np.float32)
    ncb_p = np.empty((2, 8, 2, D), np.float32)
    nca_s = np.empty((2, 128, 30, D), np.float32)
    ncb_s = np.empty((2, 128, 2, D), np.float32)
    for i in range(NCORE):
        r = outs[i]
        yT = np.asarray(r["yT"]).reshape(NG, 128, DC, TG)
        for g in range(NG):
            y_prompt[i, 1024 * g:1024 * g + 1024] = yT[g, :, :, 0:1024].transpose(2, 1, 0).reshape(1024, D)
            ys = yT[g, :, :, 1024:].reshape(128, DC, 8, 8)
            y_sample[16 * i + 8 * g:16 * i + 8 * g + 8] = ys.transpose(3, 2, 1, 0).reshape(8, 8, D)
        a = np.asarray(r["nca_p"]).reshape(2, 128, DC, 30)
        nca_p[:, i] = a.transpose(0, 3, 2, 1).reshape(2, 30, D)
        a = np.asarray(r["ncb_p"]).reshape(2, 128, DC, 2)
        ncb_p[:, i] = a.transpose(0, 3, 2, 1).reshape(2, 2, D)
        a = np.asarray(r["nca_s"]).reshape(2, NG, 128, DC, 30, 8)
        nca_s[:, 16 * i:16 * i + 16] = a.transpose(0, 1, 5, 4, 3, 2).reshape(2, 16, 30, D)
        a = np.asarray(r["ncb_s"]).reshape(2, NG, 128, DC, 2, 8)
        ncb_s[:, 16 * i:16 * i + 16] = a.transpose(0, 1, 5, 4, 3, 2).reshape(2, 16, 2, D)
    return (y_prompt, y_sample, nca_p, ncb_p, nca_s, ncb_s)
```
